# Optimizing a Trainium2 kernel written in Bass

```python
import jax, jax.numpy as jnp
from jax import lax
import numpy as np

D_MODEL = 2048
BATCH = 4
SEQ = 4096
DEPTH = 1

MIX_WIDTH = D_MODEL
FOURIER_WIDTH = MIX_WIDTH // 2
N_FOURIER_GROUPS = 4
FOURIER_GROUP_DIM = FOURIER_WIDTH // N_FOURIER_GROUPS
ATTN_WIDTH = MIX_WIDTH - FOURIER_WIDTH
HEAD_DIM = 128
N_HEADS = ATTN_WIDTH // HEAD_DIM
WINDOW_DILATIONS = ((128, 1), (512, 4), (2048, 16))
D_FF = 4 * D_MODEL
IN_WIDTH = FOURIER_WIDTH + 3 * ATTN_WIDTH
RMS_EPS = 1e-6

kernel_name = "hybrid_fourier_dilated_alibi_block"


def _rmsnorm(x, g):
    xf = x.astype(jnp.float32)
    inv = lax.rsqrt(jnp.mean(xf * xf, axis=-1, keepdims=True) + RMS_EPS)
    return (xf * inv * g.astype(jnp.float32)).astype(x.dtype)


def _alibi_slopes(n_heads):
    return jnp.asarray(2.0 ** (-8.0 * (np.arange(n_heads) + 1) / n_heads), dtype=jnp.float32)


def _dilated_branch(q, k, v, slopes, dilation, radius):
    B, H, S, Dh = q.shape
    M = S // dilation
    blk = radius
    nb = -(-M // blk)
    Mp = nb * blk

    def to_classes(t):
        return t.reshape(B, H, M, dilation, Dh).transpose(0, 1, 3, 2, 4)

    pad0 = ((0, 0), (0, 0), (0, 0))
    qc = jnp.pad(to_classes(q), pad0 + ((0, Mp - M), (0, 0))).reshape(B, H, dilation, nb, blk, Dh)
    kc = jnp.pad(to_classes(k), pad0 + ((blk, Mp - M + blk), (0, 0))).reshape(B, H, dilation, nb + 2, blk, Dh)
    vc = jnp.pad(to_classes(v), pad0 + ((blk, Mp - M + blk), (0, 0))).reshape(B, H, dilation, nb + 2, blk, Dh)
    kw = jnp.concatenate([kc[:, :, :, :-2], kc[:, :, :, 1:-1], kc[:, :, :, 2:]], axis=4)
    vw = jnp.concatenate([vc[:, :, :, :-2], vc[:, :, :, 1:-1], vc[:, :, :, 2:]], axis=4)

    mq = jnp.arange(nb)[:, None, None] * blk + jnp.arange(blk)[None, :, None]
    mk = jnp.arange(nb)[:, None, None] * blk - blk + jnp.arange(3 * blk)[None, None, :]
    rel = mk - mq
    valid = (jnp.abs(rel) <= radius) & (mk >= 0) & (mk < M)
    dist = (dilation * jnp.abs(rel)).astype(jnp.float32)

    s = jnp.einsum('bhrnqd,bhrnkd->bhrnqk', qc, kw).astype(jnp.float32)
    s = s - slopes[None, :, None, None, None, None] * dist
    s = jnp.where(valid, s, -jnp.inf)
    mx = jnp.max(s, axis=-1, keepdims=True)
    p = jnp.exp(s - mx)
    den = jnp.sum(p, axis=-1)
    o = jnp.einsum('bhrnqk,bhrnkd->bhrnqd', p.astype(v.dtype), vw).astype(jnp.float32) / den[..., None]

    def from_classes(t):
        tail = t.shape[5:]
        t = t.reshape((B, H, dilation, Mp) + tail)[:, :, :, :M]
        t = jnp.moveaxis(t, 2, 3)
        return t.reshape((B, H, S) + tail)

    return from_classes(o), from_classes(mx[..., 0]), from_classes(den)


def _dilated_attention(q, k, v):
    slopes = _alibi_slopes(q.shape[1])
    outs, maxes, dens = [], [], []
    for window, dil in WINDOW_DILATIONS:
        o, m, d = _dilated_branch(q, k, v, slopes, dil, (window // 2) // dil)
        outs.append(o); maxes.append(m); dens.append(d)
    mx = jnp.stack(maxes)
    w = jnp.stack(dens) * jnp.exp(mx - jnp.max(mx, axis=0, keepdims=True))
    o = jnp.sum(w[..., None] * jnp.stack(outs), axis=0) / jnp.sum(w, axis=0)[..., None]
    return o


def _fourier_mix(u, w_f):
    B, S, _ = u.shape
    ug = u.reshape(B, S, N_FOURIER_GROUPS, FOURIER_GROUP_DIM).astype(jnp.float32)
    re = jnp.fft.fft2(ug, axes=(1, 3), norm="ortho").real.astype(u.dtype)
    y = jnp.einsum('bsgc,gce->bsge', re, w_f)
    return y.reshape(B, S, FOURIER_WIDTH)


def setup_inputs(seed: int = 0) -> dict:
    key = jax.random.key(seed)
    ks = jax.random.split(key, 10)
    f32 = jnp.float32
    x = jax.random.normal(ks[0], (BATCH, SEQ, D_MODEL), f32)
    norm_mix_g = 1.0 + 0.02 * jax.random.normal(ks[1], (DEPTH, D_MODEL), f32)
    w_in = jax.random.normal(ks[2], (DEPTH, D_MODEL, IN_WIDTH), f32) * D_MODEL ** -0.5
    w_fourier = jax.random.normal(ks[3], (DEPTH, N_FOURIER_GROUPS, FOURIER_GROUP_DIM, FOURIER_GROUP_DIM), f32) * FOURIER_GROUP_DIM ** -0.5
    w_out = jax.random.normal(ks[4], (DEPTH, MIX_WIDTH, D_MODEL), f32) * MIX_WIDTH ** -0.5
    norm_mlp_g = 1.0 + 0.02 * jax.random.normal(ks[5], (DEPTH, D_MODEL), f32)
    w_up = jax.random.normal(ks[6], (DEPTH, D_MODEL, D_FF), f32) * D_MODEL ** -0.5
    w_down = jax.random.normal(ks[7], (DEPTH, D_FF, D_MODEL), f32) * D_FF ** -0.5
    norm_final_g = 1.0 + 0.02 * jax.random.normal(ks[8], (D_MODEL,), f32)
    return {"x": x, "norm_mix_g": norm_mix_g, "w_in": w_in, "w_fourier": w_fourier,
            "w_out": w_out, "norm_mlp_g": norm_mlp_g, "w_up": w_up, "w_down": w_down,
            "norm_final_g": norm_final_g}


def reference(x, norm_mix_g, w_in, w_fourier, w_out, norm_mlp_g, w_up, w_down, norm_final_g):
    B, S, _ = x.shape
    h = x
    for layer in range(DEPTH):
        u = _rmsnorm(h, norm_mix_g[layer])
        proj = jnp.einsum('bsd,de->bse', u, w_in[layer])
        u_f = proj[..., :FOURIER_WIDTH]
        qkv = proj[..., FOURIER_WIDTH:].reshape(B, S, 3, N_HEADS, HEAD_DIM)
        q = jnp.transpose(qkv[:, :, 0], (0, 2, 1, 3)) * (HEAD_DIM ** -0.5)
        k = jnp.transpose(qkv[:, :, 1], (0, 2, 1, 3))
        v = jnp.transpose(qkv[:, :, 2], (0, 2, 1, 3))
        y_f = _fourier_mix(u_f, w_fourier[layer])
        y_a = _dilated_attention(q, k, v).astype(h.dtype)
        y_a = jnp.transpose(y_a, (0, 2, 1, 3)).reshape(B, S, ATTN_WIDTH)
        y = jnp.concatenate([y_f, y_a], axis=-1)
        h = h + jnp.einsum('bse,ed->bsd', y, w_out[layer])
        u = _rmsnorm(h, norm_mlp_g[layer])
        a = jnp.einsum('bsd,df->bsf', u, w_up[layer])
        a = jnp.square(jax.nn.relu(a))
        h = h + jnp.einsum('bsf,fd->bsd', a, w_down[layer])
    return _rmsnorm(h, norm_final_g)
```

```python
import os
import numpy as np
import ml_dtypes
from contextlib import ExitStack
import concourse.bass as bass
import concourse.mybir as mybir
from concourse.bass_utils import run_bass_kernel_spmd

F32 = mybir.dt.float32
BF16 = mybir.dt.bfloat16
ALU = mybir.AluOpType
AF = mybir.ActivationFunctionType

D = 2048
S = 4096
NOWN = 2048
NKV = 3072
DFF = 8192
EPS = 1e-6
QSCALE = 128.0 ** -0.5
ENGS = ["pe", "act", "dve", "pool", "sp"]
ARENA_ELEMS = 100 * 1024


class Op:
    __slots__ = ("eng", "fn", "dma", "deps", "signal", "semkey", "val", "waits")


class Sched:
    def __init__(self):
        self.ops = {e: [] for e in ENGS}
        self.lastw = {}
        self.rd = {}
        self.since = []

    def add(self, eng, fn, r=(), w=(), dma=None, waw=True):
        o = Op()
        o.eng = eng
        o.fn = fn
        o.dma = dma
        o.deps = []
        o.signal = dma is not None
        o.semkey = None
        o.val = 0
        o.waits = []
        for t in r:
            lw = self.lastw.get(t)
            if lw is not None:
                o.deps.append((lw, 0))
        for t in w:
            lw = self.lastw.get(t)
            if lw is not None and waw:
                o.deps.append((lw, 1))
            rr = self.rd.get(t)
            if rr:
                for x in rr.values():
                    o.deps.append((x, 2))
        key = eng if dma is None else ("d", dma)
        for t in r:
            self.rd.setdefault(t, {})[key] = o
        for t in w:
            self.lastw[t] = o
            self.rd[t] = {}
        self.ops[eng].append(o)
        self.since.append(o)
        return o

    def barrier(self):
        last = {}
        for o in self.since:
            key = o.eng if o.dma is None else ("d", o.dma)
            last[key] = o
        self.lastw = {}
        self.rd = {}
        self.since = []
        for e in ENGS:
            o = self.add(e, None)
            for p in last.values():
                o.deps.append((p, 0))

    def finalize(self):
        for e in ENGS:
            for o in self.ops[e]:
                seen = set()
                for (p, kind) in o.deps:
                    if p is o or id(p) in seen:
                        continue
                    if p.dma is None and o.dma is None and p.eng == o.eng:
                        if e == "pe" or kind != 0:
                            continue
                    seen.add(id(p))
                    p.signal = True
                    o.waits.append(p)
        dcnt = {}
        for e in ENGS:
            cnt = 0
            for o in self.ops[e]:
                if o.dma is not None:
                    dcnt[o.dma] = dcnt.get(o.dma, 0) + 1
                    o.semkey = ("d", o.dma)
                    o.val = 16 * dcnt[o.dma]
                elif o.signal:
                    cnt += 1
                    o.semkey = e
                    o.val = cnt
        return sorted(dcnt.keys())

    def emit(self, eng_name, eng, sems):
        seen = {}
        for o in self.ops[eng_name]:
            for p in o.waits:
                if seen.get(p.semkey, 0) >= p.val:
                    continue
                eng.wait_ge(sems[p.semkey], p.val)
                seen[p.semkey] = p.val
            if o.fn is not None:
                ins = o.fn(eng)
                if o.signal:
                    ins.then_inc(sems[o.semkey], 16 if o.dma is not None else 1)
            elif o.signal:
                eng.nop().then_inc(sems[o.semkey], 1)


class Arena:
    def __init__(self, ap, base=0):
        self.ap = ap
        self.base = base
        self.off = base

    def reset(self):
        self.off = self.base

    def alloc(self, shape, dtype):
        n = 1
        for s in shape[1:]:
            n *= s
        ne = n * (2 if dtype == F32 else 1)
        ne = (ne + 31) // 32 * 32
        assert self.off + ne <= ARENA_ELEMS, (self.off, ne)
        v = self.ap[:, self.off:self.off + ne]
        self.off += ne
        if dtype == F32:
            v = v.bitcast(F32)
        v = v[:, 0:n]
        if len(shape) == 3:
            v = v.rearrange("p (a b) -> p a b", a=shape[1])
        elif len(shape) == 4:
            v = v.rearrange("p (a b c) -> p a b c", a=shape[1], b=shape[2])
        return v


def build_nc(nphase=4, debug=False):
    nc = bass.Bass("TRN2", target_bir_lowering=False)
    skind = "ExternalOutput" if debug else "Internal"

    def din(name, shape, dt):
        return nc.dram_tensor(name, shape, dt, kind="ExternalInput").ap()

    x = din("x", [S, D], F32)
    g_mix = din("g_mix", [1, D], F32)
    g_mlp = din("g_mlp", [1, D], F32)
    g_fin = din("g_fin", [1, D], F32)
    w_in = din("w_in", [D, 4096], F32)
    w_f = din("w_f", [4, 256, 256], F32)
    w_out = din("w_out", [D, D], F32)
    w_up = din("w_up", [D, DFF], F32)
    w_down = din("w_down", [DFF, D], F32)
    ident_d = din("ident", [128, 128], BF16)
    ones_d = din("ones", [128, 128], BF16)
    cs_d = din("cs_ch", [256, 512], BF16)
    dftc = din("dftc", [S, NOWN], BF16)
    dfts = din("dfts", [S, NOWN], BF16)
    alibi = din("alibi", [128, 24 * 256], F32)
    out = nc.dram_tensor("out", [NOWN, D], F32, kind="ExternalOutput").ap()

    Zscr = nc.dram_tensor("Zscr", [2, 128, 4, 2, 32, 128], BF16, kind=skind).ap()
    QTs = nc.dram_tensor("QTs", [8, 128, NOWN], BF16, kind=skind).ap()
    KTs = nc.dram_tensor("KTs", [8, 128, NKV], BF16, kind=skind).ap()
    Vs = nc.dram_tensor("Vs", [4096, 1024], BF16, kind=skind).ap()
    yTs = nc.dram_tensor("yTs", [16, 128, NOWN], BF16, kind=skind).ap()

    sch = Sched()
    es = ExitStack()
    arena_t = es.enter_context(nc.sbuf_tensor("arena", [128, ARENA_ELEMS], BF16))
    cst_t = es.enter_context(nc.sbuf_tensor("cst", [128, 256], BF16))
    ps_all = es.enter_context(nc.psum_tensor("ps", [128, 8 * 512], F32))
    A = Arena(arena_t)
    ident = cst_t[:, 0:128]
    ones = cst_t[:, 128:256]

    def psb(b):
        return ps_all[:, b * 512:(b + 1) * 512]

    def pstv(b):
        return psb(b).bitcast(BF16).rearrange("p (a b) -> p a b", b=128)

    def dma(eng, out_ap, in_ap, r, w, key, waw=True, **kw):
        return sch.add(eng, lambda e, o=out_ap, i=in_ap, kw=kw: e.dma_start(out=o, in_=i, **kw),
                       r=r, w=w, dma=key, waw=waw)

    def mm_group(outp, pairs, r, w, skip=False, start=True, stop=True):
        n = len(pairs)
        for i, (l, rh) in enumerate(pairs):
            st = start and i == 0
            sp_ = stop and i == n - 1
            rr = r if (i == 0 or i == n - 1) else ()
            ww = w if (i == 0 or i == n - 1) else ()
            sch.add("pe", lambda e, o=outp, l=l, rh=rh, st=st, sp_=sp_, sk=skip:
                    e.matmul(o, lhsT=l, rhs=rh, start=st, stop=sp_, skip_group_check=sk), r=rr, w=ww)

    evac_rr = [0]

    def evac_copy(out_ap, in_ap, r, w, scale=None, eng=None):
        if eng is None:
            eng = "act" if evac_rr[0] % 2 == 0 else "dve"
            evac_rr[0] += 1
        if eng == "act":
            if scale is None:
                sch.add("act", lambda e, o=out_ap, i=in_ap: e.activation(out=o, in_=i, func=AF.Copy), r=r, w=w)
            else:
                sch.add("act", lambda e, o=out_ap, i=in_ap, s=scale: e.activation(out=o, in_=i, func=AF.Copy, scale=s), r=r, w=w)
        else:
            if scale is None:
                sch.add("dve", lambda e, o=out_ap, i=in_ap: e.tensor_copy(out=o, in_=i), r=r, w=w)
            else:
                sch.add("dve", lambda e, o=out_ap, i=in_ap, s=scale: e.tensor_scalar(out=o, in0=i, scalar1=s, scalar2=None, op0=ALU.mult), r=r, w=w)

    dma("sp", ident, ident_d[:, :], (), [("ident",)], "c0")
    dma("sp", ones, ones_d[:, :], (), [("ones",)], "c1")

    def norm_and_transpose(xin, xtok, gtile, ss, sd, rinv, idx, ubuf, ubtok, junk, uT, utok_fn, col0):
        sch.add("act", lambda e, o=junk, i=xin, a=ss[:, idx:idx + 1]: e.activation(out=o, in_=i, func=AF.Square, accum_out=a),
                r=[xtok], w=[("junk",), ("ss", idx)])
        sch.add("act", lambda e, o=sd[:, idx:idx + 1], i=ss[:, idx:idx + 1]: e.activation(out=o, in_=i, func=AF.Sqrt, bias=EPS, scale=1.0 / D),
                r=[("ss", idx)], w=[("sd", idx)])
        sch.add("dve", lambda e, o=rinv[:, idx:idx + 1], i=sd[:, idx:idx + 1]: e.reciprocal(out=o, in_=i),
                r=[("sd", idx)], w=[("rinv", idx)])
        sch.add("dve", lambda e, o=ubuf, i=xin, s=rinv[:, idx:idx + 1], g=gtile: e.scalar_tensor_tensor(out=o, in0=i, scalar=s, in1=g, op0=ALU.mult, op1=ALU.mult),
                r=[xtok, ("rinv", idx), ("g",)], w=[ubtok])
        for b in range(2):
            for q in range(8):
                kc = b * 8 + q
                sch.add("pe", lambda e, o=pstv(b)[:, q, :], i=ubuf[:, kc * 128:(kc + 1) * 128]: e.transpose(o, i, ident),
                        r=[ubtok, ("ident",)] if q in (0, 7) else (), w=[("ps", b)] if q in (0, 7) else ())
            evac_copy(uT[:, b * 8:(b + 1) * 8, col0:col0 + 128], pstv(b), r=[("ps", b)], w=[utok_fn(b)])

    if nphase >= 1:
        A.reset()
        grep = A.alloc([128, D], F32)
        cs = A.alloc([128, 2, 512], BF16)
        xt = [A.alloc([128, D], F32) for _ in range(2)]
        junk = A.alloc([128, D], BF16)
        ss = A.alloc([128, 32], F32)
        sd = A.alloc([128, 32], F32)
        rinv = A.alloc([128, 32], F32)
        ub = [A.alloc([128, D], BF16) for _ in range(2)]
        uT = A.alloc([128, 16, 1024], BF16)
        wb = [A.alloc([128, 16, 512], BF16) for _ in range(2)]
        XTb = [A.alloc([128, 4, 1024], BF16) for _ in range(2)]
        Zt = A.alloc([128, 8, 2, 512], BF16)
        Vb = [A.alloc([128, 8, 512], BF16) for _ in range(2)]

        dma("sp", grep, g_mix.partition_broadcast(128).rearrange("p a d -> p (a d)"), (), [("g",)], "c2")
        dma("sp", cs, cs_d.rearrange("(q p) c -> p q c", p=128), (), [("cs",)], "c3")
        w_in_v = w_in.rearrange("(kc p) c -> p kc c", p=128)
        bi = 0
        xs_i = 0
        vs_i = 0
        acc_i = 0
        z_i = 0
        for TI in range(4):
            for j in range(8):
                sidx = TI * 8 + j
                sl = sidx % 2
                dma("sp", xt[sl], x[sidx * 128:(sidx + 1) * 128, :], (), [("xt", sl)], "xt%d" % sl)
                norm_and_transpose(xt[sl], ("xt", sl), grep, ss, sd, rinv, sidx, ub[sl], ("ub", sl), junk,
                                   uT, lambda b, j=j: ("uT", j, b), j * 128)
            blocks = [("F", 0), ("F", 1)]
            if TI < 2:
                blocks += [("Q", 0), ("Q", 1)]
            if TI < 3:
                blocks += [("K", 0), ("K", 1), ("V", 0), ("V", 1)]
            for (kind, hb) in blocks:
                c0 = {"F": 0, "Q": 1024, "K": 2048, "V": 3072}[kind] + hb * 512
                ws = bi % 2
                bi += 1
                dma("pool", wb[ws], w_in_v[:, :, c0:c0 + 512], (), [("wb", ws)], "wb%d" % ws)
                if kind in ("F", "Q", "K"):
                    xs = xs_i % 2
                    xs_i += 1
                    for cc in range(4):
                        for th in range(2):
                            pb = 2 + acc_i % 4
                            acc_i += 1
                            pairs = [(wb[ws][:, kc, cc * 128:(cc + 1) * 128], uT[:, kc, th * 512:(th + 1) * 512]) for kc in range(16)]
                            rtok = [("wb", ws)] + [("uT", jj, b) for jj in range(th * 4, th * 4 + 4) for b in range(2)]
                            mm_group(psb(pb), pairs, r=rtok, w=[("ps", pb)])
                            evac_copy(XTb[xs][:, cc, th * 512:(th + 1) * 512], psb(pb), r=[("ps", pb)], w=[("XTb", xs, cc, th)],
                                      scale=(QSCALE if kind == "Q" else None))
                    if kind == "F":
                        for j in range(8):
                            for gl in range(2):
                                zb_ = 6 + z_i % 2
                                z_i += 1
                                pairs = [(XTb[xs][:, 2 * gl + q, j * 128:(j + 1) * 128], cs[:, q, :]) for q in range(2)]
                                rtok = [("XTb", xs, 2 * gl + q, j // 4) for q in range(2)] + [("cs",)]
                                mm_group(psb(zb_), pairs, r=rtok, w=[("ps", zb_)])
                                evac_copy(Zt[:, j, gl, :], psb(zb_), r=[("ps", zb_)], w=[("Zt", j, gl)])
                        for gl in range(2):
                            g = 2 * hb + gl
                            for ab in range(2):
                                for half in range(2):
                                    src = Zt[:, :, gl, ab * 256 + half * 128:ab * 256 + (half + 1) * 128]
                                    dst = Zscr[ab, :, g, half, TI * 8:(TI + 1) * 8, :]
                                    dma("sp", dst, src, [("Zt", j, gl) for j in range(8)], [("Zscr", ab, g, half, TI)],
                                        "zst%d" % (gl * 4 + ab * 2 + half))
                    else:
                        tgt = QTs if kind == "Q" else KTs
                        dst = tgt[hb * 4:(hb + 1) * 4, :, TI * 1024:(TI + 1) * 1024].rearrange("h p t -> p h t")
                        dma("sp", dst, XTb[xs], [("XTb", xs, cc, th) for cc in range(4) for th in range(2)],
                            [(kind, hb, TI)], "qst%d" % xs)
                else:
                    vs = vs_i % 2
                    vs_i += 1
                    for j in range(8):
                        pb = 2 + acc_i % 4
                        acc_i += 1
                        pairs = [(uT[:, kc, j * 128:(j + 1) * 128], wb[ws][:, kc, :]) for kc in range(16)]
                        rtok = [("wb", ws), ("uT", j, 0), ("uT", j, 1)]
                        mm_group(psb(pb), pairs, r=rtok, w=[("ps", pb)])
                        evac_copy(Vb[vs][:, j, :], psb(pb), r=[("ps", pb)], w=[("Vb", vs, j)])
                    dst = Vs[TI * 1024:(TI + 1) * 1024, hb * 512:(hb + 1) * 512].rearrange("(j p) c -> p j c", p=128)
                    dma("sp", dst, Vb[vs], [("Vb", vs, j) for j in range(8)], [("Vs", hb, TI)], "vst%d" % vs)
        sch.barrier()

    if nphase >= 2:
        A.reset()
        dm = [A.alloc([128, 64, 512], BF16) for _ in range(2)]
        zb = [A.alloc([128, 64, 128], BF16) for _ in range(2)]
        wf = A.alloc([128, 4, 2, 256], BF16)
        reT = [A.alloc([128, 2, 512], BF16) for _ in range(2)]
        yst = [A.alloc([128, 512], BF16) for _ in range(4)]
        dma("pool", wf, w_f.rearrange("g (kc p) e -> p g kc e", p=128), (), [("wf",)], "wf")
        dftc_v = dftc.rearrange("(sc p) t -> p sc t", p=128)
        dfts_v = dfts.rearrange("(sc p) t -> p sc t", p=128)
        acc_i = 0
        y_i = 0
        w_i = 0
        for tb in range(4):
            ds = tb % 2
            dma("sp", dm[ds][:, 0:32, :], dftc_v[:, :, tb * 512:(tb + 1) * 512], (), [("dm", ds, 0)], "dmc%d" % ds)
            dma("sp", dm[ds][:, 32:64, :], dfts_v[:, :, tb * 512:(tb + 1) * 512], (), [("dm", ds, 1)], "dms%d" % ds)
            for cch in range(8):
                g, half = cch // 2, cch % 2
                zs = (tb * 8 + cch) % 2
                dma("sp", zb[zs][:, 0:32, :], Zscr[0, :, g, half, :, :], (), [("zb", zs, 0)], "zba%d" % zs)
                dma("sp", zb[zs][:, 32:64, :], Zscr[1, :, g, half, :, :], (), [("zb", zs, 1)], "zbb%d" % zs)
                pb = acc_i % 4
                acc_i += 1
                pairs = [(zb[zs][:, k, :], dm[ds][:, k, :]) for k in range(64)]
                mm_group(psb(pb), pairs, r=[("zb", zs, 0), ("zb", zs, 1), ("dm", ds, 0), ("dm", ds, 1)], w=[("ps", pb)])
                rs = (tb * 4 + g) % 2
                evac_copy(reT[rs][:, half, :], psb(pb), r=[("ps", pb)], w=[("reT", rs, half)])
                if half == 1:
                    for e_ in range(2):
                        pb2 = 4 + w_i % 2
                        w_i += 1
                        pairs = [(wf[:, g, kc, e_ * 128:(e_ + 1) * 128], reT[rs][:, kc, :]) for kc in range(2)]
                        mm_group(psb(pb2), pairs, r=[("wf",), ("reT", rs, 0), ("reT", rs, 1)], w=[("ps", pb2)])
                        ys = y_i % 4
                        y_i += 1
                        evac_copy(yst[ys], psb(pb2), r=[("ps", pb2)], w=[("yst", ys)])
                        dma("sp", yTs[g * 2 + e_, :, tb * 512:(tb + 1) * 512], yst[ys], [("yst", ys)], [("yTs", g * 2 + e_, tb)], "yst%d" % ys)
        sch.barrier()

    if nphase >= 3:
        A.reset()
        qT = [A.alloc([128, NOWN], BF16) for _ in range(2)]
        kT = [A.alloc([128, NKV], BF16) for _ in range(2)]
        v1 = [A.alloc([128, 17, 128], BF16) for _ in range(2)]
        v4 = [A.alloc([128, 4, 6, 128], BF16) for _ in range(2)]
        v16 = [A.alloc([128, 16, 2, 128], BF16) for _ in range(2)]
        bias = [A.alloc([128, 3, 256], F32) for _ in range(2)]
        tmp = [A.alloc([128, 256], F32) for _ in range(3)]
        pT = [A.alloc([128, 256], BF16) for _ in range(3)]
        Oacc = A.alloc([128, NOWN], F32)
        Dacc = A.alloc([128, NOWN], F32)
        rden = A.alloc([128, NOWN], F32)
        ysta = [A.alloc([128, NOWN], BF16) for _ in range(2)]
        u_i = 0
        g_i = 0
        DIL = [1, 4, 16]
        for h in range(8):
            hs = h % 2
            hc = slice(h * 128, (h + 1) * 128)
            dma("sp", qT[hs], QTs[h, :, :], (), [("qT", hs)], "aq%d" % hs)
            dma("sp", kT[hs], KTs[h, :, :], (), [("kT", hs)], "ak%d" % hs)
            dma("sp", v1[hs], Vs.rearrange("(blk p) c -> p blk c", p=128)[:, 0:17, hc], (), [("v1", hs)], "av1%d" % hs)
            dma("sp", v4[hs], Vs[0:3072, :].rearrange("(blk p r) c -> p r blk c", p=128, r=4)[:, :, :, hc], (), [("v4", hs)], "av4%d" % hs)
            dma("sp", v16[hs], Vs.rearrange("(blk p r) c -> p r blk c", p=128, r=16)[:, :, :, hc], (), [("v16", hs)], "av16%d" % hs)
            dma("sp", bias[hs], alibi[:, h * 768:(h + 1) * 768].rearrange("p (a b) -> p a b", a=3), (), [("bias", hs)], "ab%d" % hs)
            for di, d in enumerate(DIL):
                nq_cls = NOWN // d
                mkeys = nq_cls + 64
                csz = min(512, nq_cls)
                for r_ in range(d):
                    for c0 in range(0, nq_cls, csz):
                        c1 = c0 + csz
                        gs = g_i % 2
                        g_i += 1
                        ob, db_ = 4 + gs, 6 + gs
                        kb0 = max(0, (c0 - 64) // 128)
                        kb1 = (c1 - 1 + 64) // 128
                        units = []
                        for kb in range(kb0, kb1 + 1):
                            k0 = kb * 128
                            k1 = min(k0 + 128, mkeys)
                            if k1 <= k0:
                                continue
                            qa = max(c0, k0 - 64)
                            qb = min(c1, k0 + 192)
                            if qb <= qa:
                                continue
                            units.append((kb, k0, k1, qa, qb))
                        for ui, (kb, k0, k1, qa, qb) in enumerate(units):
                            nk = k1 - k0
                            nq = qb - qa
                            joff = qa - (k0 - 64)
                            us = u_i % 3
                            sbk = u_i % 4
                            u_i += 1
                            kap = kT[hs][:, r_ + d * k0: r_ + d * (k1 - 1) + 1: d]
                            qap = qT[hs][:, r_ + d * qa: r_ + d * (qb - 1) + 1: d]
                            mm_group(psb(sbk)[0:nk, 0:nq], [(kap, qap)], r=[("kT", hs), ("qT", hs)], w=[("ps", sbk)])
                            sch.add("dve", lambda e, o=tmp[us][0:nk, 0:nq], i=psb(sbk)[0:nk, 0:nq], b=bias[hs][0:nk, di, joff:joff + nq]:
                                    e.tensor_tensor(out=o, in0=i, in1=b, op=ALU.add),
                                    r=[("ps", sbk), ("bias", hs)], w=[("tmp", us)])
                            sch.add("act", lambda e, o=pT[us][0:nk, 0:nq], i=tmp[us][0:nk, 0:nq]: e.activation(out=o, in_=i, func=AF.Exp),
                                    r=[("tmp", us)], w=[("pT", us)])
                            if d == 1:
                                vap = v1[hs][0:nk, kb, :]
                                vtok = ("v1", hs)
                            elif d == 4:
                                vap = v4[hs][0:nk, r_, kb, :]
                                vtok = ("v4", hs)
                            else:
                                vap = v16[hs][0:nk, r_, kb, :]
                                vtok = ("v16", hs)
                            first = ui == 0
                            last = ui == len(units) - 1
                            mm_group(psb(ob)[:, qa - c0:qb - c0], [(vap, pT[us][0:nk, 0:nq])], r=[vtok, ("pT", us)], w=[("ps", ob)],
                                     skip=True, start=first, stop=last)
                            mm_group(psb(db_)[:, qa - c0:qb - c0], [(ones[0:nk, :], pT[us][0:nk, 0:nq])], r=[("ones",), ("pT", us)], w=[("ps", db_)],
                                     skip=True, start=first, stop=last)
                        t0 = r_ + d * c0
                        t1 = r_ + d * (c1 - 1) + 1
                        otok = [("Oacc", t) for t in range(c0 * d // 512, (c1 * d - 1) // 512 + 1)]
                        dtok = [("Dacc", t) for t in range(c0 * d // 512, (c1 * d - 1) // 512 + 1)]
                        if d == 1:
                            sch.add("act", lambda e, o=Oacc[:, t0:t1], i=psb(ob)[:, 0:csz]: e.activation(out=o, in_=i, func=AF.Copy),
                                    r=[("ps", ob)], w=otok)
                            sch.add("dve", lambda e, o=Dacc[:, t0:t1], i=psb(db_)[:, 0:csz]: e.tensor_copy(out=o, in_=i),
                                    r=[("ps", db_)], w=dtok)
                        else:
                            sch.add("dve", lambda e, o=Oacc[:, t0:t1:d], i=psb(ob)[:, 0:csz]: e.tensor_tensor(out=o, in0=i, in1=o, op=ALU.add),
                                    r=[("ps", ob)] + otok, w=otok)
                            sch.add("dve", lambda e, o=Dacc[:, t0:t1:d], i=psb(db_)[:, 0:csz]: e.tensor_tensor(out=o, in0=i, in1=o, op=ALU.add),
                                    r=[("ps", db_)] + dtok, w=dtok)
            alltok_o = [("Oacc", t) for t in range(4)]
            alltok_d = [("Dacc", t) for t in range(4)]
            sch.add("dve", lambda e, o=rden, i=Dacc: e.reciprocal(out=o, in_=i), r=alltok_d, w=[("rden",)])
            sch.add("dve", lambda e, o=ysta[hs], i=Oacc, b=rden: e.tensor_tensor(out=o, in0=i, in1=b, op=ALU.mult),
                    r=alltok_o + [("rden",)], w=[("ysta", hs)])
            dma("sp", yTs[8 + h, :, :], ysta[hs], [("ysta", hs)], [("yTs", 8 + h)], "ay%d" % hs)
        sch.barrier()

    if nphase >= 4:
        A.reset()
        gbuf = A.alloc([128, D], F32)
        hres = A.alloc([128, 8, D], F32)
        uT4 = A.alloc([128, 16, 1024], BF16)
        wA = [A.alloc([128, 16, 512], BF16) for _ in range(2)]
        wBf = [A.alloc([128, 4 * D], BF16) for _ in range(2)]
        wB = [t.rearrange("p (a b) -> p a b", a=4) for t in wBf]
        aT = [A.alloc([128, 4, 1024], BF16) for _ in range(2)]
        ub4 = [A.alloc([128, D], BF16) for _ in range(2)]
        rl = [A.alloc([128, 512], F32) for _ in range(2)]
        ss4 = A.alloc([128, 32], F32)
        sd4 = A.alloc([128, 32], F32)
        rinv4 = A.alloc([128, 32], F32)
        junk4 = aT[1].rearrange("p a b -> p (a b)")[:, 0:D]
        w_out_v = w_out.rearrange("(kc p) c -> p kc c", p=128)
        w_up_v = w_up.rearrange("(kc p) c -> p kc c", p=128)
        w_down_v = w_down.rearrange("(fc p) c -> p fc c", p=128)
        wa_i = 0
        wb_i = 0
        acc_i = 0
        up_i = 0
        rl_i = 0
        for P in range(2):
            for sl in range(2):
                dst = wBf[sl].rearrange("p (a b) -> p a b", a=8)
                src = yTs[sl * 8:(sl + 1) * 8, :, P * 1024:(P + 1) * 1024].rearrange("c p t -> p c t")
                dma("sp", dst, src, (), [("wB", sl)], "yl%d" % sl)
            dma("sp", hres, x[P * 1024:(P + 1) * 1024, :].rearrange("(j p) c -> p j c", p=128), (),
                [("h", j, db) for j in range(8) for db in range(4)], "xh")
            dma("sp", gbuf, g_mlp.partition_broadcast(128).rearrange("p a d -> p (a d)"), (), [("g",)], "g4")

            def yT_ap(e_, j):
                sl, q = e_ // 8, e_ % 8
                return wBf[sl].rearrange("p (a b) -> p a b", a=8)[:, q, j * 128:(j + 1) * 128]

            for db in range(4):
                ws = wa_i % 2
                wa_i += 1
                dma("pool", wA[ws], w_out_v[:, :, db * 512:(db + 1) * 512], (), [("wA", ws)], "wA%d" % ws)
                for j in range(8):
                    pb = 4 + acc_i % 4
                    acc_i += 1
                    pairs = [(yT_ap(e_, j), wA[ws][:, e_, :]) for e_ in range(16)]
                    mm_group(psb(pb), pairs, r=[("wA", ws), ("wB", 0), ("wB", 1)], w=[("ps", pb)])
                    hv = hres[:, j, db * 512:(db + 1) * 512]
                    sch.add("dve", lambda e, o=hv, i=psb(pb): e.tensor_tensor(out=o, in0=i, in1=o, op=ALU.add),
                            r=[("ps", pb), ("h", j, db)], w=[("h", j, db)])
            for j in range(8):
                idx = P * 8 + j
                sl = idx % 2
                sch.add("act", lambda e, o=junk4, i=hres[:, j, :], a=ss4[:, idx:idx + 1]: e.activation(out=o, in_=i, func=AF.Square, accum_out=a),
                        r=[("h", j, db) for db in range(4)], w=[("aT", 1, fc, th) for fc in range(4) for th in range(2)] + [("ss", idx)])
                sch.add("act", lambda e, o=sd4[:, idx:idx + 1], i=ss4[:, idx:idx + 1]: e.activation(out=o, in_=i, func=AF.Sqrt, bias=EPS, scale=1.0 / D),
                        r=[("ss", idx)], w=[("sd", idx)])
                sch.add("dve", lambda e, o=rinv4[:, idx:idx + 1], i=sd4[:, idx:idx + 1]: e.reciprocal(out=o, in_=i),
                        r=[("sd", idx)], w=[("rinv", idx)])
                sch.add("dve", lambda e, o=ub4[sl], i=hres[:, j, :], s=rinv4[:, idx:idx + 1], g=gbuf: e.scalar_tensor_tensor(out=o, in0=i, scalar=s, in1=g, op0=ALU.mult, op1=ALU.mult),
                        r=[("h", j, db) for db in range(4)] + [("rinv", idx), ("g",)], w=[("ub", sl)])
                for b in range(2):
                    for q in range(8):
                        kc = b * 8 + q
                        sch.add("pe", lambda e, o=pstv(b)[:, q, :], i=ub4[sl][:, kc * 128:(kc + 1) * 128]: e.transpose(o, i, ident),
                                r=[("ub", sl), ("ident",)] if q in (0, 7) else (), w=[("ps", b)] if q in (0, 7) else ())
                    evac_copy(uT4[:, b * 8:(b + 1) * 8, j * 128:(j + 1) * 128], pstv(b), r=[("ps", b)], w=[("uT", j, b)])
            dma("sp", gbuf, g_fin.partition_broadcast(128).rearrange("p a d -> p (a d)"), (), [("g",)], "g4")

            def emit_up(fb, wsA, as_):
                nonlocal up_i, rl_i
                for fc in range(4):
                    for th in range(2):
                        pb = 2 + up_i % 2
                        up_i += 1
                        pairs = [(wA[wsA][:, kc, fc * 128:(fc + 1) * 128], uT4[:, kc, th * 512:(th + 1) * 512]) for kc in range(16)]
                        rtok = [("wA", wsA)] + [("uT", jj, b) for jj in range(th * 4, th * 4 + 4) for b in range(2)]
                        mm_group(psb(pb), pairs, r=rtok, w=[("ps", pb)])
                        rs = rl_i % 2
                        rl_i += 1
                        sch.add("act", lambda e, o=rl[rs], i=psb(pb): e.activation(out=o, in_=i, func=AF.Relu),
                                r=[("ps", pb)], w=[("rl", rs)])
                        sch.add("dve", lambda e, o=aT[as_][:, fc, th * 512:(th + 1) * 512], i=rl[rs]: e.tensor_tensor(out=o, in0=i, in1=i, op=ALU.mult),
                                r=[("rl", rs)], w=[("aT", as_, fc, th)])

            def emit_down(fb, wsB, as_):
                nonlocal acc_i
                for j in range(8):
                    for db in range(4):
                        pb = 4 + acc_i % 4
                        acc_i += 1
                        pairs = [(aT[as_][:, fc, j * 128:(j + 1) * 128], wB[wsB][:, fc, db * 512:(db + 1) * 512]) for fc in range(4)]
                        rtok = [("wB", wsB)] + [("aT", as_, fc, j // 4) for fc in range(4)]
                        mm_group(psb(pb), pairs, r=rtok, w=[("ps", pb)])
                        hv = hres[:, j, db * 512:(db + 1) * 512]
                        sch.add("dve", lambda e, o=hv, i=psb(pb): e.tensor_tensor(out=o, in0=i, in1=o, op=ALU.add),
                                r=[("ps", pb), ("h", j, db)], w=[("h", j, db)])

            NFB = 16
            slots = []
            for fb in range(NFB):
                wsA = wa_i % 2
                wa_i += 1
                wsB = wb_i % 2
                wb_i += 1
                slots.append((wsA, wsB, fb % 2))

            def load_w(fb):
                wsA, wsB, _ = slots[fb]
                dma("pool", wA[wsA], w_up_v[:, :, fb * 512:(fb + 1) * 512], (), [("wA", wsA)], "wA%d" % wsA)
                for hh in range(2):
                    dma("pool", wB[wsB][:, :, hh * 1024:(hh + 1) * 1024], w_down_v[:, fb * 4:(fb + 1) * 4, hh * 1024:(hh + 1) * 1024],
                        (), [("wB", wsB)], "wB%d" % wsB, waw=(hh == 0))

            load_w(0)
            emit_up(0, slots[0][0], slots[0][2])
            for fb in range(NFB):
                if fb + 1 < NFB:
                    load_w(fb + 1)
                    emit_up(fb + 1, slots[fb + 1][0], slots[fb + 1][2])
                emit_down(fb, slots[fb][1], slots[fb][2])
            for j in range(8):
                idx = 16 + P * 8 + j
                sch.add("act", lambda e, o=junk4, i=hres[:, j, :], a=ss4[:, idx:idx + 1]: e.activation(out=o, in_=i, func=AF.Square, accum_out=a),
                        r=[("h", j, db) for db in range(4)], w=[("aT", 1, fc, th) for fc in range(4) for th in range(2)] + [("ss", idx)])
                sch.add("act", lambda e, o=sd4[:, idx:idx + 1], i=ss4[:, idx:idx + 1]: e.activation(out=o, in_=i, func=AF.Sqrt, bias=EPS, scale=1.0 / D),
                        r=[("ss", idx)], w=[("sd", idx)])
                sch.add("dve", lambda e, o=rinv4[:, idx:idx + 1], i=sd4[:, idx:idx + 1]: e.reciprocal(out=o, in_=i),
                        r=[("sd", idx)], w=[("rinv", idx)])
                sch.add("dve", lambda e, o=hres[:, j, :], s=rinv4[:, idx:idx + 1], g=gbuf: e.scalar_tensor_tensor(out=o, in0=o, scalar=s, in1=g, op0=ALU.mult, op1=ALU.mult),
                        r=[("h", j, db) for db in range(4)] + [("rinv", idx), ("g",)], w=[("h", j, db) for db in range(4)])
                dma("sp", out[P * 1024 + j * 128:P * 1024 + (j + 1) * 128, :], hres[:, j, :], [("h", j, db) for db in range(4)],
                    [("out", P, j)], "ost%d" % (j % 2))
        sch.barrier()
    elif not debug:
        raise ValueError("nphase<4 requires debug")

    if nphase < 4:
        pass

    dkeys = sch.finalize()
    sems = {}
    for e in ENGS:
        sems[e] = es.enter_context(nc.semaphore("s_" + e))
    for k in dkeys:
        sems[("d", k)] = es.enter_context(nc.semaphore("d_" + k))
    block = es.enter_context(nc.Block())

    @block.tensor
    def _(e):
        sch.emit("pe", e, sems)

    @block.scalar
    def _(e):
        sch.emit("act", e, sems)

    @block.vector
    def _(e):
        sch.emit("dve", e, sems)

    @block.gpsimd
    def _(e):
        sch.emit("pool", e, sems)

    @block.sync
    def _(e):
        sch.emit("sp", e, sems)

    es.close()
    return nc


def _consts():
    bf = ml_dtypes.bfloat16
    ident = np.eye(128, dtype=np.float32).astype(bf)
    ones = np.ones((128, 128), dtype=np.float32).astype(bf)
    c = np.arange(256)
    ang = 2.0 * np.pi * ((c[:, None] * c[None, :]) % 256) / 256.0
    cs = np.concatenate([np.cos(ang), np.sin(ang)], axis=1).astype(np.float32).astype(bf)
    tab = np.arange(S, dtype=np.float64) * (2.0 * np.pi / S)
    ctab = np.cos(tab) / 1024.0
    stab = -np.sin(tab) / 1024.0
    dft = []
    for hf in range(2):
        sl = np.arange(S, dtype=np.int64)
        tl = np.arange(NOWN, dtype=np.int64)
        if hf == 1:
            sl = S - 1 - sl
            tl = S - 1 - tl
        idx = (sl[:, None] * tl[None, :]) % S
        dft.append((ctab[idx].astype(np.float32).astype(bf), stab[idx].astype(np.float32).astype(bf)))
    i = np.arange(128)[:, None]
    j = np.arange(256)[None, :]
    rel = i - j + 64
    valid = np.abs(rel) <= 64
    al = np.zeros((128, 24, 256), dtype=np.float32)
    for h in range(8):
        slope = 2.0 ** (-8.0 * (h + 1) / 8.0)
        for di, d in enumerate((1, 4, 16)):
            al[:, h * 3 + di, :] = np.where(valid, -slope * d * np.abs(rel), -30000.0)
    return ident, ones, cs, dft, al.reshape(128, 24 * 256)


_NC_CACHE = {}


def kernel(x, norm_mix_g, w_in, w_fourier, w_out, norm_mlp_g, w_up, w_down, norm_final_g):
    x = np.asarray(x, dtype=np.float32)
    ident, ones, cs, dft, al = _consts()
    if "nc" not in _NC_CACHE:
        _NC_CACHE["nc"] = build_nc(4, False)
    nc = _NC_CACHE["nc"]
    shared = {
        "g_mix": np.ascontiguousarray(np.asarray(norm_mix_g, np.float32).reshape(1, D)),
        "g_mlp": np.ascontiguousarray(np.asarray(norm_mlp_g, np.float32).reshape(1, D)),
        "g_fin": np.ascontiguousarray(np.asarray(norm_final_g, np.float32).reshape(1, D)),
        "w_in": np.ascontiguousarray(np.asarray(w_in, np.float32)[0]),
        "w_f": np.ascontiguousarray(np.asarray(w_fourier, np.float32)[0]),
        "w_out": np.ascontiguousarray(np.asarray(w_out, np.float32)[0]),
        "w_up": np.ascontiguousarray(np.asarray(w_up, np.float32)[0]),
        "w_down": np.ascontiguousarray(np.asarray(w_down, np.float32)[0]),
        "ident": ident, "ones": ones, "cs_ch": cs, "alibi": al,
    }
    in_maps = []
    for core in range(8):
        b, hf = core // 2, core % 2
        xb = x[b] if hf == 0 else x[b, ::-1]
        m = dict(shared)
        m["x"] = np.ascontiguousarray(xb)
        m["dftc"] = dft[hf][0]
        m["dfts"] = dft[hf][1]
        in_maps.append(m)
    res = run_bass_kernel_spmd(nc, in_maps, core_ids=list(range(8)))
    outp = np.empty((4, S, D), dtype=np.float32)
    for core in range(8):
        b, hf = core // 2, core % 2
        o = np.asarray(res.results[core]["out"], dtype=np.float32)
        if hf == 0:
            outp[b, 0:NOWN] = o
        else:
            outp[b, NOWN:S] = o[::-1]
    return outp
```

```python
import os
import numpy as np
import ml_dtypes
from contextlib import ExitStack
import concourse.bass as bass
import concourse.mybir as mybir
from concourse.bass_utils import run_bass_kernel_spmd

F32 = mybir.dt.float32
BF16 = mybir.dt.bfloat16
ALU = mybir.AluOpType
AF = mybir.ActivationFunctionType

D = 2048
S = 4096
NOWN = 2048
NKV = 3072
DFF = 8192
EPS = 1e-6
QSCALE = 128.0 ** -0.5
ENGS = ["pe", "act", "dve", "pool", "sp"]
ARENA_ELEMS = 100 * 1024


class Op:
    __slots__ = ("eng", "fn", "dma", "deps", "signal", "semkey", "val", "waits")


class Sched:
    def __init__(self):
        self.ops = {e: [] for e in ENGS}
        self.lastw = {}
        self.rd = {}
        self.since = []

    def add(self, eng, fn, r=(), w=(), dma=None, waw=True):
        o = Op()
        o.eng = eng
        o.fn = fn
        o.dma = dma
        o.deps = []
        o.signal = dma is not None
        o.semkey = None
        o.val = 0
        o.waits = []
        for t in r:
            lw = self.lastw.get(t)
            if lw is not None:
                o.deps.append((lw, 0))
        for t in w:
            lw = self.lastw.get(t)
            if lw is not None and waw:
                o.deps.append((lw, 1))
            rr = self.rd.get(t)
            if rr:
                for x in rr.values():
                    o.deps.append((x, 2))
        key = eng if dma is None else ("d", dma)
        for t in r:
            self.rd.setdefault(t, {})[key] = o
        for t in w:
            self.lastw[t] = o
            self.rd[t] = {}
        self.ops[eng].append(o)
        self.since.append(o)
        return o

    def barrier(self):
        last = {}
        for o in self.since:
            key = o.eng if o.dma is None else ("d", o.dma)
            last[key] = o
        self.lastw = {}
        self.rd = {}
        self.since = []
        for e in ENGS:
            o = self.add(e, None)
            for p in last.values():
                o.deps.append((p, 0))

    def finalize(self):
        for e in ENGS:
            for o in self.ops[e]:
                seen = set()
                for (p, kind) in o.deps:
                    if p is o or id(p) in seen:
                        continue
                    if p.dma is None and o.dma is None and p.eng == o.eng:
                        if e == "pe" or kind != 0:
                            continue
                    seen.add(id(p))
                    p.signal = True
                    o.waits.append(p)
        dcnt = {}
        for e in ENGS:
            cnt = 0
            for o in self.ops[e]:
                if o.dma is not None:
                    dcnt[o.dma] = dcnt.get(o.dma, 0) + 1
                    o.semkey = ("d", o.dma)
                    o.val = 16 * dcnt[o.dma]
                elif o.signal:
                    cnt += 1
                    o.semkey = e
                    o.val = cnt
        return sorted(dcnt.keys())

    def emit(self, eng_name, eng, sems):
        seen = {}
        for o in self.ops[eng_name]:
            for p in o.waits:
                if seen.get(p.semkey, 0) >= p.val:
                    continue
                eng.wait_ge(sems[p.semkey], p.val)
                seen[p.semkey] = p.val
            if o.fn is not None:
                ins = o.fn(eng)
                if o.signal:
                    ins.then_inc(sems[o.semkey], 16 if o.dma is not None else 1)
            elif o.signal:
                eng.nop().then_inc(sems[o.semkey], 1)


class Arena:
    def __init__(self, ap, base=0):
        self.ap = ap
        self.base = base
        self.off = base

    def reset(self):
        self.off = self.base

    def alloc(self, shape, dtype):
        n = 1
        for s in shape[1:]:
            n *= s
        ne = n * (2 if dtype == F32 else 1)
        ne = (ne + 31) // 32 * 32
        assert self.off + ne <= ARENA_ELEMS, (self.off, ne)
        v = self.ap[:, self.off:self.off + ne]
        self.off += ne
        if dtype == F32:
            v = v.bitcast(F32)
        v = v[:, 0:n]
        if len(shape) == 3:
            v = v.rearrange("p (a b) -> p a b", a=shape[1])
        elif len(shape) == 4:
            v = v.rearrange("p (a b c) -> p a b c", a=shape[1], b=shape[2])
        return v


def build_nc(nphase=4, debug=False):
    nc = bass.Bass("TRN2", target_bir_lowering=False)
    skind = "ExternalOutput" if debug else "Internal"

    def din(name, shape, dt):
        return nc.dram_tensor(name, shape, dt, kind="ExternalInput").ap()

    x = din("x", [S, D], F32)
    g_mix = din("g_mix", [1, D], F32)
    g_mlp = din("g_mlp", [1, D], F32)
    g_fin = din("g_fin", [1, D], F32)
    w_in = din("w_in", [D, 4096], F32)
    w_f = din("w_f", [4, 256, 256], F32)
    w_out = din("w_out", [D, D], F32)
    w_up = din("w_up", [D, DFF], F32)
    w_down = din("w_down", [DFF, D], F32)
    ident_d = din("ident", [128, 128], BF16)
    ones_d = din("ones", [128, 128], BF16)
    cs_d = din("cs_ch", [256, 512], BF16)
    dftc = din("dftc", [S, NOWN], BF16)
    dfts = din("dfts", [S, NOWN], BF16)
    alibi = din("alibi", [128, 24 * 256], F32)
    out = nc.dram_tensor("out", [NOWN, D], F32, kind="ExternalOutput").ap()

    Zscr = nc.dram_tensor("Zscr", [2, 128, 4, 2, 32, 128], BF16, kind=skind).ap()
    QTs = nc.dram_tensor("QTs", [8, 128, NOWN], BF16, kind=skind).ap()
    KTs = nc.dram_tensor("KTs", [8, 128, NKV], BF16, kind=skind).ap()
    Vs = nc.dram_tensor("Vs", [4096, 1024], BF16, kind=skind).ap()
    yTs = nc.dram_tensor("yTs", [16, 128, NOWN], BF16, kind=skind).ap()

    sch = Sched()
    es = ExitStack()
    arena_t = es.enter_context(nc.sbuf_tensor("arena", [128, ARENA_ELEMS], BF16))
    cst_t = es.enter_context(nc.sbuf_tensor("cst", [128, 256], BF16))
    ps_all = es.enter_context(nc.psum_tensor("ps", [128, 8 * 512], F32))
    A = Arena(arena_t)
    ident = cst_t[:, 0:128]
    ones = cst_t[:, 128:256]

    def psb(b):
        return ps_all[:, b * 512:(b + 1) * 512]

    def pstv(b):
        return psb(b).bitcast(BF16).rearrange("p (a b) -> p a b", b=128)

    def dma(eng, out_ap, in_ap, r, w, key, waw=True, **kw):
        return sch.add(eng, lambda e, o=out_ap, i=in_ap, kw=kw: e.dma_start(out=o, in_=i, **kw),
                       r=r, w=w, dma=key, waw=waw)

    def mm_group(outp, pairs, r, w, skip=False, start=True, stop=True):
        n = len(pairs)
        for i, (l, rh) in enumerate(pairs):
            st = start and i == 0
            sp_ = stop and i == n - 1
            rr = r if (i == 0 or i == n - 1) else ()
            ww = w if (i == 0 or i == n - 1) else ()
            sch.add("pe", lambda e, o=outp, l=l, rh=rh, st=st, sp_=sp_, sk=skip:
                    e.matmul(o, lhsT=l, rhs=rh, start=st, stop=sp_, skip_group_check=sk), r=rr, w=ww)

    evac_rr = [0]

    def evac_copy(out_ap, in_ap, r, w, scale=None, eng=None):
        if eng is None:
            eng = "act" if evac_rr[0] % 2 == 0 else "dve"
            evac_rr[0] += 1
        if eng == "act":
            if scale is None:
                sch.add("act", lambda e, o=out_ap, i=in_ap: e.activation(out=o, in_=i, func=AF.Copy), r=r, w=w)
            else:
                sch.add("act", lambda e, o=out_ap, i=in_ap, s=scale: e.activation(out=o, in_=i, func=AF.Copy, scale=s), r=r, w=w)
        else:
            if scale is None:
                sch.add("dve", lambda e, o=out_ap, i=in_ap: e.tensor_copy(out=o, in_=i), r=r, w=w)
            else:
                sch.add("dve", lambda e, o=out_ap, i=in_ap, s=scale: e.tensor_scalar(out=o, in0=i, scalar1=s, scalar2=None, op0=ALU.mult), r=r, w=w)

    dma("sp", ident, ident_d[:, :], (), [("ident",)], "c0")
    dma("sp", ones, ones_d[:, :], (), [("ones",)], "c1")

    def norm_and_transpose(xin, xtok, gtile, ss, sd, rinv, idx, ubuf, ubtok, junk, uT, utok_fn, col0):
        sch.add("act", lambda e, o=junk, i=xin, a=ss[:, idx:idx + 1]: e.activation(out=o, in_=i, func=AF.Square, accum_out=a),
                r=[xtok], w=[("junk",), ("ss", idx)])
        sch.add("act", lambda e, o=sd[:, idx:idx + 1], i=ss[:, idx:idx + 1]: e.activation(out=o, in_=i, func=AF.Sqrt, bias=EPS, scale=1.0 / D),
                r=[("ss", idx)], w=[("sd", idx)])
        sch.add("dve", lambda e, o=rinv[:, idx:idx + 1], i=sd[:, idx:idx + 1]: e.reciprocal(out=o, in_=i),
                r=[("sd", idx)], w=[("rinv", idx)])
        sch.add("dve", lambda e, o=ubuf, i=xin, s=rinv[:, idx:idx + 1], g=gtile: e.scalar_tensor_tensor(out=o, in0=i, scalar=s, in1=g, op0=ALU.mult, op1=ALU.mult),
                r=[xtok, ("rinv", idx), ("g",)], w=[ubtok])
        for b in range(2):
            for q in range(8):
                kc = b * 8 + q
                sch.add("pe", lambda e, o=pstv(b)[:, q, :], i=ubuf[:, kc * 128:(kc + 1) * 128]: e.transpose(o, i, ident),
                        r=[ubtok, ("ident",)] if q in (0, 7) else (), w=[("ps", b)] if q in (0, 7) else ())
            evac_copy(uT[:, b * 8:(b + 1) * 8, col0:col0 + 128], pstv(b), r=[("ps", b)], w=[utok_fn(b)])

    if nphase >= 1:
        A.reset()
        grep = A.alloc([128, D], F32)
        cs = A.alloc([128, 2, 512], BF16)
        xt = [A.alloc([128, D], F32) for _ in range(2)]
        junk = A.alloc([128, D], BF16)
        ss = A.alloc([128, 32], F32)
        sd = A.alloc([128, 32], F32)
        rinv = A.alloc([128, 32], F32)
        ub = [A.alloc([128, D], BF16) for _ in range(2)]
        uT2 = [A.alloc([128, 16, 1024], BF16) for _ in range(2)]
        wb = [A.alloc([128, 16, 512], BF16) for _ in range(2)]
        XTb = [A.alloc([128, 4, 1024], BF16) for _ in range(2)]
        Zt = A.alloc([128, 8, 2, 512], BF16)
        Vb = [A.alloc([128, 8, 512], BF16) for _ in range(2)]

        dma("sp", grep, g_mix.partition_broadcast(128).rearrange("p a d -> p (a d)"), (), [("g",)], "c2")
        dma("sp", cs, cs_d.rearrange("(q p) c -> p q c", p=128), (), [("cs",)], "c3")
        w_in_v = w_in.rearrange("(kc p) c -> p kc c", p=128)
        bi = 0
        xs_i = 0
        vs_i = 0
        acc_i = 0
        z_i = 0

        def do_norm(TI, j):
            sidx = TI * 8 + j
            sl = sidx % 2
            us_ = TI % 2
            dma("sp", xt[sl], x[sidx * 128:(sidx + 1) * 128, :], (), [("xt", sl)], "xt%d" % sl)
            norm_and_transpose(xt[sl], ("xt", sl), grep, ss, sd, rinv, sidx, ub[sl], ("ub", sl), junk,
                               uT2[us_], lambda b, j=j, us_=us_: ("uT", us_, j, b), j * 128)

        for j in range(8):
            do_norm(0, j)
        for TI in range(4):
            uT = uT2[TI % 2]
            ut_ = TI % 2
            blocks = [("F", 0), ("F", 1)]
            if TI < 2:
                blocks += [("Q", 0), ("Q", 1)]
            if TI < 3:
                blocks += [("K", 0), ("K", 1), ("V", 0), ("V", 1)]
            nblk = len(blocks)
            for bidx, (kind, hb) in enumerate(blocks):
                c0 = {"F": 0, "Q": 1024, "K": 2048, "V": 3072}[kind] + hb * 512
                ws = bi % 2
                bi += 1
                dma("pool", wb[ws], w_in_v[:, :, c0:c0 + 512], (), [("wb", ws)], "wb%d" % ws)
                if kind in ("F", "Q", "K"):
                    xs = xs_i % 2
                    xs_i += 1
                    for cc in range(4):
                        for th in range(2):
                            pb = 2 + acc_i % 4
                            acc_i += 1
                            pairs = [(wb[ws][:, kc, cc * 128:(cc + 1) * 128], uT[:, kc, th * 512:(th + 1) * 512]) for kc in range(16)]
                            rtok = [("wb", ws)] + [("uT", ut_, jj, b) for jj in range(th * 4, th * 4 + 4) for b in range(2)]
                            mm_group(psb(pb), pairs, r=rtok, w=[("ps", pb)])
                            evac_copy(XTb[xs][:, cc, th * 512:(th + 1) * 512], psb(pb), r=[("ps", pb)], w=[("XTb", xs, cc, th)],
                                      scale=(QSCALE if kind == "Q" else None))
                    if kind == "F":
                        for j in range(8):
                            for gl in range(2):
                                zb_ = 6 + z_i % 2
                                z_i += 1
                                pairs = [(XTb[xs][:, 2 * gl + q, j * 128:(j + 1) * 128], cs[:, q, :]) for q in range(2)]
                                rtok = [("XTb", xs, 2 * gl + q, j // 4) for q in range(2)] + [("cs",)]
                                mm_group(psb(zb_), pairs, r=rtok, w=[("ps", zb_)])
                                evac_copy(Zt[:, j, gl, :], psb(zb_), r=[("ps", zb_)], w=[("Zt", j, gl)])
                        for gl in range(2):
                            g = 2 * hb + gl
                            for ab in range(2):
                                for half in range(2):
                                    src = Zt[:, :, gl, ab * 256 + half * 128:ab * 256 + (half + 1) * 128]
                                    dst = Zscr[ab, :, g, half, TI * 8:(TI + 1) * 8, :]
                                    dma("act", dst, src, [("Zt", j, gl) for j in range(8)], [("Zscr", ab, g, half, TI)],
                                        "zst%d" % (gl * 4 + ab * 2 + half))
                    else:
                        tgt = QTs if kind == "Q" else KTs
                        dst = tgt[hb * 4:(hb + 1) * 4, :, TI * 1024:(TI + 1) * 1024].rearrange("h p t -> p h t")
                        dma("act", dst, XTb[xs], [("XTb", xs, cc, th) for cc in range(4) for th in range(2)],
                            [(kind, hb, TI)], "qst%d" % xs)
                else:
                    vs = vs_i % 2
                    vs_i += 1
                    for j in range(8):
                        pb = 2 + acc_i % 4
                        acc_i += 1
                        pairs = [(uT[:, kc, j * 128:(j + 1) * 128], wb[ws][:, kc, :]) for kc in range(16)]
                        rtok = [("wb", ws), ("uT", ut_, j, 0), ("uT", ut_, j, 1)]
                        mm_group(psb(pb), pairs, r=rtok, w=[("ps", pb)])
                        evac_copy(Vb[vs][:, j, :], psb(pb), r=[("ps", pb)], w=[("Vb", vs, j)])
                    dst = Vs[TI * 1024:(TI + 1) * 1024, hb * 512:(hb + 1) * 512].rearrange("(j p) c -> p j c", p=128)
                    dma("act", dst, Vb[vs], [("Vb", vs, j) for j in range(8)], [("Vs", hb, TI)], "vst%d" % vs)
                if TI < 3:
                    for j in range(bidx * 8 // nblk, (bidx + 1) * 8 // nblk):
                        do_norm(TI + 1, j)
        sch.barrier()

    if nphase >= 2:
        A.reset()
        dm = [A.alloc([128, 64, 512], BF16) for _ in range(2)]
        zb = [A.alloc([128, 64, 128], BF16) for _ in range(2)]
        wf = A.alloc([128, 4, 2, 256], BF16)
        reT = [A.alloc([128, 2, 512], BF16) for _ in range(2)]
        yst = [A.alloc([128, 512], BF16) for _ in range(4)]
        dma("pool", wf, w_f.rearrange("g (kc p) e -> p g kc e", p=128), (), [("wf",)], "wf")
        dftc_v = dftc.rearrange("(sc p) t -> p sc t", p=128)
        dfts_v = dfts.rearrange("(sc p) t -> p sc t", p=128)
        acc_i = 0
        y_i = 0
        w_i = 0
        for tb in range(4):
            ds = tb % 2
            dma("sp", dm[ds][:, 0:32, :], dftc_v[:, :, tb * 512:(tb + 1) * 512], (), [("dm", ds, 0)], "dmc%d" % ds)
            dma("sp", dm[ds][:, 32:64, :], dfts_v[:, :, tb * 512:(tb + 1) * 512], (), [("dm", ds, 1)], "dms%d" % ds)
            for cch in range(8):
                g, half = cch // 2, cch % 2
                zs = (tb * 8 + cch) % 2
                dma("sp", zb[zs][:, 0:32, :], Zscr[0, :, g, half, :, :], (), [("zb", zs, 0)], "zba%d" % zs)
                dma("sp", zb[zs][:, 32:64, :], Zscr[1, :, g, half, :, :], (), [("zb", zs, 1)], "zbb%d" % zs)
                pb = acc_i % 4
                acc_i += 1
                pairs = [(zb[zs][:, k, :], dm[ds][:, k, :]) for k in range(64)]
                mm_group(psb(pb), pairs, r=[("zb", zs, 0), ("zb", zs, 1), ("dm", ds, 0), ("dm", ds, 1)], w=[("ps", pb)])
                rs = (tb * 4 + g) % 2
                evac_copy(reT[rs][:, half, :], psb(pb), r=[("ps", pb)], w=[("reT", rs, half)])
                if half == 1:
                    for e_ in range(2):
                        pb2 = 4 + w_i % 2
                        w_i += 1
                        pairs = [(wf[:, g, kc, e_ * 128:(e_ + 1) * 128], reT[rs][:, kc, :]) for kc in range(2)]
                        mm_group(psb(pb2), pairs, r=[("wf",), ("reT", rs, 0), ("reT", rs, 1)], w=[("ps", pb2)])
                        ys = y_i % 4
                        y_i += 1
                        evac_copy(yst[ys], psb(pb2), r=[("ps", pb2)], w=[("yst", ys)])
                        dma("pool", yTs[g * 2 + e_, :, tb * 512:(tb + 1) * 512], yst[ys], [("yst", ys)], [("yTs", g * 2 + e_, tb)], "yst%d" % ys)
        sch.barrier()

    if nphase >= 3:
        A.reset()
        qT = [A.alloc([128, NOWN], BF16) for _ in range(2)]
        kT = [A.alloc([128, NKV], BF16) for _ in range(2)]
        v1 = [A.alloc([128, 17, 128], BF16) for _ in range(2)]
        v4 = [A.alloc([128, 4, 6, 128], BF16) for _ in range(2)]
        v16 = [A.alloc([128, 16, 2, 128], BF16) for _ in range(2)]
        bias = [A.alloc([128, 3, 256], F32) for _ in range(2)]
        tmp = [A.alloc([128, 256], F32) for _ in range(3)]
        pT = [A.alloc([128, 256], BF16) for _ in range(3)]
        Oacc = A.alloc([128, NOWN], F32)
        Dacc = A.alloc([128, NOWN], F32)
        rden = A.alloc([128, NOWN], F32)
        ysta = [A.alloc([128, NOWN], BF16) for _ in range(2)]
        DIL = [1, 4, 16]
        LA = 4
        NB = 7
        tmp = tmp + [A.alloc([128, 256], F32) for _ in range(NB - len(tmp))]
        pT = pT + [A.alloc([128, 256], BF16) for _ in range(NB - len(pT))]

        def head_loads(h):
            hs = h % 2
            hc = slice(h * 128, (h + 1) * 128)
            dma("sp", qT[hs], QTs[h, :, :], (), [("qT", hs)], "aq%d" % hs)
            dma("sp", kT[hs], KTs[h, :, :], (), [("kT", hs)], "ak%d" % hs)
            dma("sp", v1[hs], Vs.rearrange("(blk p) c -> p blk c", p=128)[:, 0:17, hc], (), [("v1", hs)], "av1%d" % hs)
            dma("sp", v4[hs], Vs[0:3072, :].rearrange("(blk p r) c -> p r blk c", p=128, r=4)[:, :, :, hc], (), [("v4", hs)], "av4%d" % hs)
            dma("sp", v16[hs], Vs.rearrange("(blk p r) c -> p r blk c", p=128, r=16)[:, :, :, hc], (), [("v16", hs)], "av16%d" % hs)
            dma("sp", bias[hs], alibi[:, h * 768:(h + 1) * 768].rearrange("p (a b) -> p a b", a=3), (), [("bias", hs)], "ab%d" % hs)

        U = []
        g_i = 0
        for h in range(8):
            first_in_head = True
            for di, d in enumerate(DIL):
                nq_cls = NOWN // d
                mkeys = nq_cls + 64
                csz = min(512, nq_cls)
                for r_ in range(d):
                    for c0 in range(0, nq_cls, csz):
                        c1 = c0 + csz
                        gs = g_i % 2
                        g_i += 1
                        kb0 = max(0, (c0 - 64) // 128)
                        kb1 = (c1 - 1 + 64) // 128
                        units = []
                        for kb in range(kb0, kb1 + 1):
                            k0 = kb * 128
                            k1 = min(k0 + 128, mkeys)
                            if k1 <= k0:
                                continue
                            qa = max(c0, k0 - 64)
                            qb = min(c1, k0 + 192)
                            if qb <= qa:
                                continue
                            units.append(dict(h=h, di=di, d=d, r=r_, c0=c0, c1=c1, csz=csz, gs=gs, kb=kb, k0=k0, k1=k1, qa=qa, qb=qb,
                                              first=False, last=False, head_start=False, head_end=False))
                        units[0]["first"] = True
                        units[-1]["last"] = True
                        if first_in_head:
                            units[0]["head_start"] = True
                            first_in_head = False
                        U += units
            U[-1]["head_end"] = True
        for i, u in enumerate(U):
            u["idx"] = i

        def emit_S(u):
            h, d, r_, di = u["h"], u["d"], u["r"], u["di"]
            hs = h % 2
            if u["head_start"]:
                head_loads(h)
            k0, k1, qa, qb = u["k0"], u["k1"], u["qa"], u["qb"]
            nk, nq = k1 - k0, qb - qa
            joff = qa - (k0 - 64)
            us = u["idx"] % NB
            sbk = u["idx"] % 4
            kap = kT[hs][:, r_ + d * k0: r_ + d * (k1 - 1) + 1: d]
            qap = qT[hs][:, r_ + d * qa: r_ + d * (qb - 1) + 1: d]
            mm_group(psb(sbk)[0:nk, 0:nq], [(kap, qap)], r=[("kT", hs), ("qT", hs)], w=[("ps", sbk)])
            sch.add("dve", lambda e, o=tmp[us][0:nk, 0:nq], i=psb(sbk)[0:nk, 0:nq], b=bias[hs][0:nk, di, joff:joff + nq]:
                    e.tensor_tensor(out=o, in0=i, in1=b, op=ALU.add),
                    r=[("ps", sbk), ("bias", hs)], w=[("tmp", us)])
            sch.add("act", lambda e, o=pT[us][0:nk, 0:nq], i=tmp[us][0:nk, 0:nq]: e.activation(out=o, in_=i, func=AF.Exp),
                    r=[("tmp", us)], w=[("pT", us)])

        def emit_PV(u):
            h, d, r_, di = u["h"], u["d"], u["r"], u["di"]
            hs = h % 2
            k0, k1, qa, qb, c0, c1, csz = u["k0"], u["k1"], u["qa"], u["qb"], u["c0"], u["c1"], u["csz"]
            nk, nq = k1 - k0, qb - qa
            kb = u["kb"]
            us = u["idx"] % NB
            ob, db_ = 4 + u["gs"], 6 + u["gs"]
            if d == 1:
                vap = v1[hs][0:nk, kb, :]
                vtok = ("v1", hs)
            elif d == 4:
                vap = v4[hs][0:nk, r_, kb, :]
                vtok = ("v4", hs)
            else:
                vap = v16[hs][0:nk, r_, kb, :]
                vtok = ("v16", hs)
            mm_group(psb(ob)[:, qa - c0:qb - c0], [(vap, pT[us][0:nk, 0:nq])], r=[vtok, ("pT", us)], w=[("ps", ob)],
                     skip=True, start=u["first"], stop=u["last"])
            mm_group(psb(db_)[:, qa - c0:qb - c0], [(ones[0:nk, :], pT[us][0:nk, 0:nq])], r=[("ones",), ("pT", us)], w=[("ps", db_)],
                     skip=True, start=u["first"], stop=u["last"])
            if u["last"]:
                t0 = r_ + d * c0
                t1 = r_ + d * (c1 - 1) + 1
                otok = [("Oacc", t) for t in range(c0 * d // 512, (c1 * d - 1) // 512 + 1)]
                dtok = [("Dacc", t) for t in range(c0 * d // 512, (c1 * d - 1) // 512 + 1)]
                if d == 1:
                    sch.add("act", lambda e, o=Oacc[:, t0:t1], i=psb(ob)[:, 0:csz]: e.activation(out=o, in_=i, func=AF.Copy),
                            r=[("ps", ob)], w=otok)
                    sch.add("dve", lambda e, o=Dacc[:, t0:t1], i=psb(db_)[:, 0:csz]: e.tensor_copy(out=o, in_=i),
                            r=[("ps", db_)], w=dtok)
                else:
                    sch.add("dve", lambda e, o=Oacc[:, t0:t1:d], i=psb(ob)[:, 0:csz]: e.tensor_tensor(out=o, in0=i, in1=o, op=ALU.add),
                            r=[("ps", ob)] + otok, w=otok)
                    sch.add("dve", lambda e, o=Dacc[:, t0:t1:d], i=psb(db_)[:, 0:csz]: e.tensor_tensor(out=o, in0=i, in1=o, op=ALU.add),
                            r=[("ps", db_)] + dtok, w=dtok)
            if u["head_end"]:
                alltok_o = [("Oacc", t) for t in range(4)]
                alltok_d = [("Dacc", t) for t in range(4)]
                sch.add("dve", lambda e, o=rden, i=Dacc: e.reciprocal(out=o, in_=i), r=alltok_d, w=[("rden",)])
                sch.add("dve", lambda e, o=ysta[hs], i=Oacc, b=rden: e.tensor_tensor(out=o, in0=i, in1=b, op=ALU.mult),
                        r=alltok_o + [("rden",)], w=[("ysta", hs)])
                dma("pool", yTs[8 + h, :, :], ysta[hs], [("ysta", hs)], [("yTs", 8 + h)], "ay%d" % hs)

        for i in range(len(U) + LA):
            if i < len(U):
                emit_S(U[i])
            if i - LA >= 0:
                emit_PV(U[i - LA])
        sch.barrier()

    if nphase >= 4:
        A.reset()
        gbuf = A.alloc([128, D], F32)
        hres = A.alloc([128, 8, D], F32)
        uT4 = A.alloc([128, 16, 1024], BF16)
        wA = [A.alloc([128, 16, 512], BF16) for _ in range(2)]
        wBf = [A.alloc([128, 4 * D], BF16) for _ in range(2)]
        wB = [t.rearrange("p (a b) -> p a b", a=4) for t in wBf]
        aT = [A.alloc([128, 4, 1024], BF16) for _ in range(2)]
        ub4 = [A.alloc([128, D], BF16) for _ in range(2)]
        rl = [A.alloc([128, 512], F32) for _ in range(2)]
        ss4 = A.alloc([128, 32], F32)
        sd4 = A.alloc([128, 32], F32)
        rinv4 = A.alloc([128, 32], F32)
        junk4 = aT[1].rearrange("p a b -> p (a b)")[:, 0:D]
        w_out_v = w_out.rearrange("(kc p) c -> p kc c", p=128)
        w_up_v = w_up.rearrange("(kc p) c -> p kc c", p=128)
        w_down_v = w_down.rearrange("(fc p) c -> p fc c", p=128)
        wa_i = 0
        wb_i = 0
        acc_i = 0
        up_i = 0
        rl_i = 0
        for P in range(2):
            for sl in range(2):
                dst = wBf[sl].rearrange("p (a b) -> p a b", a=8)
                src = yTs[sl * 8:(sl + 1) * 8, :, P * 1024:(P + 1) * 1024].rearrange("c p t -> p c t")
                dma("sp", dst, src, (), [("wB", sl)], "yl%d" % sl)
            dma("sp", hres, x[P * 1024:(P + 1) * 1024, :].rearrange("(j p) c -> p j c", p=128), (),
                [("h", j, db) for j in range(8) for db in range(4)], "xh")
            dma("sp", gbuf, g_mlp.partition_broadcast(128).rearrange("p a d -> p (a d)"), (), [("g",)], "g4")

            def yT_ap(e_, j):
                sl, q = e_ // 8, e_ % 8
                return wBf[sl].rearrange("p (a b) -> p a b", a=8)[:, q, j * 128:(j + 1) * 128]

            for db in range(4):
                ws = wa_i % 2
                wa_i += 1
                dma("pool", wA[ws], w_out_v[:, :, db * 512:(db + 1) * 512], (), [("wA", ws)], "wA%d" % ws)
                for j in range(8):
                    pb = 4 + acc_i % 4
                    acc_i += 1
                    pairs = [(yT_ap(e_, j), wA[ws][:, e_, :]) for e_ in range(16)]
                    mm_group(psb(pb), pairs, r=[("wA", ws), ("wB", 0), ("wB", 1)], w=[("ps", pb)])
                    hv = hres[:, j, db * 512:(db + 1) * 512]
                    sch.add("dve", lambda e, o=hv, i=psb(pb): e.tensor_tensor(out=o, in0=i, in1=o, op=ALU.add),
                            r=[("ps", pb), ("h", j, db)], w=[("h", j, db)])
            for j in range(8):
                idx = P * 8 + j
                sl = idx % 2
                sch.add("act", lambda e, o=junk4, i=hres[:, j, :], a=ss4[:, idx:idx + 1]: e.activation(out=o, in_=i, func=AF.Square, accum_out=a),
                        r=[("h", j, db) for db in range(4)], w=[("aT", 1, fc, th) for fc in range(4) for th in range(2)] + [("ss", idx)])
                sch.add("act", lambda e, o=sd4[:, idx:idx + 1], i=ss4[:, idx:idx + 1]: e.activation(out=o, in_=i, func=AF.Sqrt, bias=EPS, scale=1.0 / D),
                        r=[("ss", idx)], w=[("sd", idx)])
                sch.add("dve", lambda e, o=rinv4[:, idx:idx + 1], i=sd4[:, idx:idx + 1]: e.reciprocal(out=o, in_=i),
                        r=[("sd", idx)], w=[("rinv", idx)])
                sch.add("dve", lambda e, o=ub4[sl], i=hres[:, j, :], s=rinv4[:, idx:idx + 1], g=gbuf: e.scalar_tensor_tensor(out=o, in0=i, scalar=s, in1=g, op0=ALU.mult, op1=ALU.mult),
                        r=[("h", j, db) for db in range(4)] + [("rinv", idx), ("g",)], w=[("ub", sl)])
                for b in range(2):
                    for q in range(8):
                        kc = b * 8 + q
                        sch.add("pe", lambda e, o=pstv(b)[:, q, :], i=ub4[sl][:, kc * 128:(kc + 1) * 128]: e.transpose(o, i, ident),
                                r=[("ub", sl), ("ident",)] if q in (0, 7) else (), w=[("ps", b)] if q in (0, 7) else ())
                    evac_copy(uT4[:, b * 8:(b + 1) * 8, j * 128:(j + 1) * 128], pstv(b), r=[("ps", b)], w=[("uT", j, b)])
            dma("sp", gbuf, g_fin.partition_broadcast(128).rearrange("p a d -> p (a d)"), (), [("g",)], "g4")

            def emit_up(fb, wsA, as_):
                nonlocal up_i, rl_i
                for fc in range(4):
                    for th in range(2):
                        pb = 2 + up_i % 2
                        up_i += 1
                        pairs = [(wA[wsA][:, kc, fc * 128:(fc + 1) * 128], uT4[:, kc, th * 512:(th + 1) * 512]) for kc in range(16)]
                        rtok = [("wA", wsA)] + [("uT", jj, b) for jj in range(th * 4, th * 4 + 4) for b in range(2)]
                        mm_group(psb(pb), pairs, r=rtok, w=[("ps", pb)])
                        rs = rl_i % 2
                        rl_i += 1
                        sch.add("act", lambda e, o=rl[rs], i=psb(pb): e.activation(out=o, in_=i, func=AF.Relu),
                                r=[("ps", pb)], w=[("rl", rs)])
                        sch.add("dve", lambda e, o=aT[as_][:, fc, th * 512:(th + 1) * 512], i=rl[rs]: e.tensor_tensor(out=o, in0=i, in1=i, op=ALU.mult),
                                r=[("rl", rs)], w=[("aT", as_, fc, th)])

            def emit_down(fb, wsB, as_):
                nonlocal acc_i
                for j in range(8):
                    for db in range(4):
                        pb = 4 + acc_i % 4
                        acc_i += 1
                        pairs = [(aT[as_][:, fc, j * 128:(j + 1) * 128], wB[wsB][:, fc, db * 512:(db + 1) * 512]) for fc in range(4)]
                        rtok = [("wB", wsB)] + [("aT", as_, fc, j // 4) for fc in range(4)]
                        mm_group(psb(pb), pairs, r=rtok, w=[("ps", pb)])
                        hv = hres[:, j, db * 512:(db + 1) * 512]
                        sch.add("dve", lambda e, o=hv, i=psb(pb): e.tensor_tensor(out=o, in0=i, in1=o, op=ALU.add),
                                r=[("ps", pb), ("h", j, db)], w=[("h", j, db)])

            NFB = 16
            slots = []
            for fb in range(NFB):
                wsA = wa_i % 2
                wa_i += 1
                wsB = wb_i % 2
                wb_i += 1
                slots.append((wsA, wsB, fb % 2))

            def load_w(fb):
                wsA, wsB, _ = slots[fb]
                dma("pool", wA[wsA], w_up_v[:, :, fb * 512:(fb + 1) * 512], (), [("wA", wsA)], "wA%d" % wsA)
                for hh in range(2):
                    dma("pool", wB[wsB][:, :, hh * 1024:(hh + 1) * 1024], w_down_v[:, fb * 4:(fb + 1) * 4, hh * 1024:(hh + 1) * 1024],
                        (), [("wB", wsB)], "wB%d" % wsB, waw=(hh == 0))

            load_w(0)
            emit_up(0, slots[0][0], slots[0][2])
            for fb in range(NFB):
                if fb + 1 < NFB:
                    load_w(fb + 1)
                    emit_up(fb + 1, slots[fb + 1][0], slots[fb + 1][2])
                emit_down(fb, slots[fb][1], slots[fb][2])
            for j in range(8):
                idx = 16 + P * 8 + j
                sch.add("act", lambda e, o=junk4, i=hres[:, j, :], a=ss4[:, idx:idx + 1]: e.activation(out=o, in_=i, func=AF.Square, accum_out=a),
                        r=[("h", j, db) for db in range(4)], w=[("aT", 1, fc, th) for fc in range(4) for th in range(2)] + [("ss", idx)])
                sch.add("act", lambda e, o=sd4[:, idx:idx + 1], i=ss4[:, idx:idx + 1]: e.activation(out=o, in_=i, func=AF.Sqrt, bias=EPS, scale=1.0 / D),
                        r=[("ss", idx)], w=[("sd", idx)])
                sch.add("dve", lambda e, o=rinv4[:, idx:idx + 1], i=sd4[:, idx:idx + 1]: e.reciprocal(out=o, in_=i),
                        r=[("sd", idx)], w=[("rinv", idx)])
                sch.add("dve", lambda e, o=hres[:, j, :], s=rinv4[:, idx:idx + 1], g=gbuf: e.scalar_tensor_tensor(out=o, in0=o, scalar=s, in1=g, op0=ALU.mult, op1=ALU.mult),
                        r=[("h", j, db) for db in range(4)] + [("rinv", idx), ("g",)], w=[("h", j, db) for db in range(4)])
                dma("sp", out[P * 1024 + j * 128:P * 1024 + (j + 1) * 128, :], hres[:, j, :], [("h", j, db) for db in range(4)],
                    [("out", P, j)], "ost%d" % (j % 2))
        sch.barrier()
    elif not debug:
        raise ValueError("nphase<4 requires debug")

    if nphase < 4:
        pass

    dkeys = sch.finalize()
    sems = {}
    for e in ENGS:
        sems[e] = es.enter_context(nc.semaphore("s_" + e))
    for k in dkeys:
        sems[("d", k)] = es.enter_context(nc.semaphore("d_" + k))
    block = es.enter_context(nc.Block())

    @block.tensor
    def _(e):
        sch.emit("pe", e, sems)

    @block.scalar
    def _(e):
        sch.emit("act", e, sems)

    @block.vector
    def _(e):
        sch.emit("dve", e, sems)

    @block.gpsimd
    def _(e):
        sch.emit("pool", e, sems)

    @block.sync
    def _(e):
        sch.emit("sp", e, sems)

    es.close()
    return nc


def _consts():
    bf = ml_dtypes.bfloat16
    ident = np.eye(128, dtype=np.float32).astype(bf)
    ones = np.ones((128, 128), dtype=np.float32).astype(bf)
    c = np.arange(256)
    ang = 2.0 * np.pi * ((c[:, None] * c[None, :]) % 256) / 256.0
    cs = np.concatenate([np.cos(ang), np.sin(ang)], axis=1).astype(np.float32).astype(bf)
    tab = np.arange(S, dtype=np.float64) * (2.0 * np.pi / S)
    ctab = np.cos(tab) / 1024.0
    stab = -np.sin(tab) / 1024.0
    dft = []
    for hf in range(2):
        sl = np.arange(S, dtype=np.int64)
        tl = np.arange(NOWN, dtype=np.int64)
        if hf == 1:
            sl = S - 1 - sl
            tl = S - 1 - tl
        idx = (sl[:, None] * tl[None, :]) % S
        dft.append((ctab[idx].astype(np.float32).astype(bf), stab[idx].astype(np.float32).astype(bf)))
    i = np.arange(128)[:, None]
    j = np.arange(256)[None, :]
    rel = i - j + 64
    valid = np.abs(rel) <= 64
    al = np.zeros((128, 24, 256), dtype=np.float32)
    for h in range(8):
        slope = 2.0 ** (-8.0 * (h + 1) / 8.0)
        for di, d in enumerate((1, 4, 16)):
            al[:, h * 3 + di, :] = np.where(valid, -slope * d * np.abs(rel), -30000.0)
    return ident, ones, cs, dft, al.reshape(128, 24 * 256)


_NC_CACHE = {}


def kernel(x, norm_mix_g, w_in, w_fourier, w_out, norm_mlp_g, w_up, w_down, norm_final_g):
    x = np.asarray(x, dtype=np.float32)
    ident, ones, cs, dft, al = _consts()
    if "nc" not in _NC_CACHE:
        _NC_CACHE["nc"] = build_nc(4, False)
    nc = _NC_CACHE["nc"]
    shared = {
        "g_mix": np.ascontiguousarray(np.asarray(norm_mix_g, np.float32).reshape(1, D)),
        "g_mlp": np.ascontiguousarray(np.asarray(norm_mlp_g, np.float32).reshape(1, D)),
        "g_fin": np.ascontiguousarray(np.asarray(norm_final_g, np.float32).reshape(1, D)),
        "w_in": np.ascontiguousarray(np.asarray(w_in, np.float32)[0]),
        "w_f": np.ascontiguousarray(np.asarray(w_fourier, np.float32)[0]),
        "w_out": np.ascontiguousarray(np.asarray(w_out, np.float32)[0]),
        "w_up": np.ascontiguousarray(np.asarray(w_up, np.float32)[0]),
        "w_down": np.ascontiguousarray(np.asarray(w_down, np.float32)[0]),
        "ident": ident, "ones": ones, "cs_ch": cs, "alibi": al,
    }
    in_maps = []
    for core in range(8):
        b, hf = core // 2, core % 2
        xb = x[b] if hf == 0 else x[b, ::-1]
        m = dict(shared)
        m["x"] = np.ascontiguousarray(xb)
        m["dftc"] = dft[hf][0]
        m["dfts"] = dft[hf][1]
        in_maps.append(m)
    res = run_bass_kernel_spmd(nc, in_maps, core_ids=list(range(8)))
    outp = np.empty((4, S, D), dtype=np.float32)
    for core in range(8):
        b, hf = core // 2, core % 2
        o = np.asarray(res.results[core]["out"], dtype=np.float32)
        if hf == 0:
            outp[b, 0:NOWN] = o
        else:
            outp[b, NOWN:S] = o[::-1]
    return outp
```
